# Optimizing a Trainium2 kernel written in Bass

```python
import jax, jax.numpy as jnp
from jax import lax
import numpy as np

D_MODEL = 2048
BATCH = 1
SEQ = 8192
DEPTH = 2

CHUNK = 64
N_MIXERS = 2
N_HGRN_LAYERS = (DEPTH + 1) // 2
N_GMLP_LAYERS = DEPTH // 2

HGRN_EXPAND = 128
HGRN_HEADS = D_MODEL // HGRN_EXPAND
FORGET_DIM = HGRN_HEADS * HGRN_EXPAND
HGRN_HEAD_V = D_MODEL // HGRN_HEADS
HGRN_IN_WIDTH = 2 * FORGET_DIM + 2 * D_MODEL

GMLP_BLOCK = 128
GMLP_HALF = D_MODEL
GMLP_GROUPS = 16
GMLP_GROUP_DIM = GMLP_HALF // GMLP_GROUPS

D_FF = 4 * D_MODEL
NORM_EPS = 1e-6

kernel_name = "hgrn2_gmlp_interleaved_trunk"


def rms_norm(x, gain):
    xf = x.astype(jnp.float32)
    y = xf * lax.rsqrt(jnp.mean(xf * xf, axis=-1, keepdims=True) + NORM_EPS)
    return (y * gain.astype(jnp.float32)).astype(x.dtype)


def layer_norm(x, gain, bias):
    xf = x.astype(jnp.float32)
    mu = jnp.mean(xf, axis=-1, keepdims=True)
    xc = xf - mu
    y = xc * lax.rsqrt(jnp.mean(xc * xc, axis=-1, keepdims=True) + NORM_EPS)
    return (y * gain.astype(jnp.float32) + bias.astype(jnp.float32)).astype(x.dtype)


def hgrn2_mixer(h, w_in, w_out, g_norm, lb):
    B, S, _ = h.shape
    proj = h @ w_in
    q, f, i, g = jnp.split(proj, [FORGET_DIM, 2 * FORGET_DIM, 2 * FORGET_DIM + D_MODEL], axis=-1)
    q = jax.nn.silu(q.astype(jnp.float32))
    forget = lb + (1.0 - lb) * jax.nn.sigmoid(f.astype(jnp.float32))
    k = 1.0 - forget
    log_f = jnp.log(forget)
    v = i.astype(jnp.float32)
    nc = S // CHUNK

    def to_chunks(t, d):
        return t.reshape(B, nc, CHUNK, HGRN_HEADS, d).transpose(1, 0, 3, 2, 4)

    qc = to_chunks(q, HGRN_EXPAND)
    kc = to_chunks(k, HGRN_EXPAND)
    gc = to_chunks(log_f, HGRN_EXPAND)
    vc = to_chunks(v, HGRN_HEAD_V)
    causal = jnp.tril(jnp.ones((CHUNK, CHUNK), dtype=bool))

    def step(state, inp):
        q_t, k_t, g_t, v_t = inp
        b = jnp.cumsum(g_t, axis=2)
        o_inter = jnp.einsum('bhtd,bhde->bhte', q_t * jnp.exp(b), state)
        rel = b[:, :, :, None, :] - b[:, :, None, :, :]
        decay = jnp.exp(jnp.where(causal[:, :, None], rel, -jnp.inf))
        scores = jnp.einsum('bhtd,bhtsd,bhsd->bhts', q_t, decay, k_t)
        o = o_inter + jnp.einsum('bhts,bhse->bhte', scores, v_t)
        b_last = b[:, :, -1:, :]
        new_state = (jnp.exp(b_last[:, :, 0, :])[..., None] * state
                     + jnp.einsum('bhsd,bhse->bhde', k_t * jnp.exp(b_last - b), v_t))
        return new_state, o

    state0 = jnp.zeros((B, HGRN_HEADS, HGRN_EXPAND, HGRN_HEAD_V), jnp.float32)
    _, o = lax.scan(step, state0, (qc, kc, gc, vc))
    o = o.transpose(1, 0, 3, 2, 4).reshape(B, S, HGRN_HEADS, HGRN_HEAD_V)
    o = o * lax.rsqrt(jnp.mean(o * o, axis=-1, keepdims=True) + NORM_EPS) * g_norm.astype(jnp.float32)
    gate = jax.nn.silu(g.astype(jnp.float32)).reshape(B, S, HGRN_HEADS, HGRN_HEAD_V)
    o = (o * gate).reshape(B, S, D_MODEL).astype(h.dtype)
    return o @ w_out


def gmlp_mixer(h, w_in, w_out, ln_gain, ln_bias, w_spatial, b_spatial):
    B, S, _ = h.shape
    z = jax.nn.gelu(h @ w_in, approximate=False)
    u, v = jnp.split(z, 2, axis=-1)
    v = layer_norm(v, ln_gain, ln_bias)
    nb = S // GMLP_BLOCK
    v = v.reshape(B, nb, GMLP_BLOCK, GMLP_GROUPS, GMLP_GROUP_DIM)
    chunk_id = jnp.arange(GMLP_BLOCK) // CHUNK
    mask = chunk_id[None, :] <= chunk_id[:, None]
    w = jnp.where(mask[None], w_spatial, 0)
    mixed = jnp.einsum('gts,bnsgc->bntgc', w, v) + b_spatial.T[None, None, :, :, None]
    out = u * mixed.reshape(B, S, GMLP_HALF)
    return out @ w_out


def sq_relu_mlp(h, w1, w2):
    a = jax.nn.relu(h @ w1)
    return (a * a) @ w2


def setup_inputs(seed: int = 0) -> dict:
    key = jax.random.key(seed)
    ks = jax.random.split(key, 16)
    f32 = jnp.float32
    nrm = lambda k, shape, scale: jax.random.normal(k, shape, f32) * scale
    return {
        "x": nrm(ks[0], (BATCH, SEQ, D_MODEL), 1.0),
        "norm_mix": 1.0 + nrm(ks[1], (DEPTH, D_MODEL), 0.05),
        "norm_mlp": 1.0 + nrm(ks[2], (DEPTH, D_MODEL), 0.05),
        "final_norm": 1.0 + nrm(ks[3], (D_MODEL,), 0.05),
        "hgrn_w_in": nrm(ks[4], (N_HGRN_LAYERS, D_MODEL, HGRN_IN_WIDTH), D_MODEL ** -0.5),
        "hgrn_w_out": nrm(ks[5], (N_HGRN_LAYERS, D_MODEL, D_MODEL), D_MODEL ** -0.5),
        "hgrn_g_norm": 1.0 + nrm(ks[6], (N_HGRN_LAYERS, HGRN_HEAD_V), 0.05),
        "hgrn_lb_logits": nrm(ks[7], (DEPTH + 1, FORGET_DIM), 0.5),
        "gmlp_w_in": nrm(ks[8], (N_GMLP_LAYERS, D_MODEL, 2 * GMLP_HALF), D_MODEL ** -0.5),
        "gmlp_w_out": nrm(ks[9], (N_GMLP_LAYERS, GMLP_HALF, D_MODEL), GMLP_HALF ** -0.5),
        "gmlp_ln_gain": 1.0 + nrm(ks[10], (N_GMLP_LAYERS, GMLP_HALF), 0.05),
        "gmlp_ln_bias": nrm(ks[11], (N_GMLP_LAYERS, GMLP_HALF), 0.02),
        "gmlp_w_spatial": nrm(ks[12], (N_GMLP_LAYERS, GMLP_GROUPS, GMLP_BLOCK, GMLP_BLOCK), GMLP_BLOCK ** -0.5),
        "gmlp_b_spatial": 1.0 + nrm(ks[13], (N_GMLP_LAYERS, GMLP_GROUPS, GMLP_BLOCK), 0.1),
        "mlp_w1": nrm(ks[14], (DEPTH, D_MODEL, D_FF), D_MODEL ** -0.5),
        "mlp_w2": nrm(ks[15], (DEPTH, D_FF, D_MODEL), D_FF ** -0.5),
    }


def reference(x, norm_mix, norm_mlp, final_norm, hgrn_w_in, hgrn_w_out, hgrn_g_norm,
              hgrn_lb_logits, gmlp_w_in, gmlp_w_out, gmlp_ln_gain, gmlp_ln_bias,
              gmlp_w_spatial, gmlp_b_spatial, mlp_w1, mlp_w2):
    lb_table = jnp.cumsum(jax.nn.softmax(hgrn_lb_logits.astype(jnp.float32), axis=0), axis=0)
    h = x
    for i in range(DEPTH):
        j = i // N_MIXERS
        hn = rms_norm(h, norm_mix[i])
        if i % N_MIXERS == 0:
            mix = hgrn2_mixer(hn, hgrn_w_in[j], hgrn_w_out[j], hgrn_g_norm[j], lb_table[i])
        else:
            mix = gmlp_mixer(hn, gmlp_w_in[j], gmlp_w_out[j], gmlp_ln_gain[j], gmlp_ln_bias[j],
                             gmlp_w_spatial[j], gmlp_b_spatial[j])
        h = h + mix.astype(h.dtype)
        h = h + sq_relu_mlp(rms_norm(h, norm_mlp[i]), mlp_w1[i], mlp_w2[i]).astype(h.dtype)
    return rms_norm(h, final_norm)
```

```python
from contextlib import ExitStack
import numpy as np
import concourse.bass as bass
import concourse.mybir as mybir
from concourse.bass_utils import run_bass_kernel_spmd

F32 = mybir.dt.float32
BF16 = mybir.dt.bfloat16
AF = mybir.ActivationFunctionType
ALU = mybir.AluOpType
AX = mybir.AxisListType

NCORES = 8
T = 1024
NT = 8
D = 2048
KC = 16
DFF = 8192
EPS = 1e-6
NB_W = 5
NS_W = 2


class Buf:
    __slots__ = ("name", "w", "r", "dsem")

    def __init__(self, name):
        self.name = name
        self.w = None
        self.r = []
        self.dsem = None


class K:
    ENG = ("pe", "act", "dve", "pool", "sp")

    def __init__(self, nc, es):
        self.nc = nc
        self.es = es
        self.eng = {"pe": nc.tensor, "act": nc.scalar, "dve": nc.vector, "pool": nc.gpsimd, "sp": nc.sync}
        self.sems = {}
        self.cnt = {}
        self.seen = {e: {} for e in self.ENG}
        for e in ("pe", "act", "dve", "pool"):
            self.sems[e] = es.enter_context(nc.semaphore("s_" + e))
            self.cnt[e] = 0
        self.ndsem = 0
        self.nwait = 0
        self.ninst = 0

    def sbuf(self, name, shape, dt):
        return self.es.enter_context(self.nc.sbuf_tensor("sb_" + name, list(shape), dt))

    def psum(self, name, shape, dt=F32):
        return self.es.enter_context(self.nc.psum_tensor(name, list(shape), dt))

    def _dsem(self, buf):
        if buf.dsem is None:
            key = "d%d" % self.ndsem
            self.ndsem += 1
            self.sems[key] = self.es.enter_context(self.nc.semaphore("s_" + key))
            self.cnt[key] = 0
            buf.dsem = key
        return buf.dsem

    def _wait(self, e, ev):
        if ev is None:
            return
        key, val = ev
        if self.seen[e].get(key, 0) >= val:
            return
        self.eng[e].wait_ge(self.sems[key], val)
        self.seen[e][key] = val
        self.nwait += 1

    def _deps(self, e, reads, writes):
        for b in reads:
            if b.w is not None and not (e == "pe" and b.w[0] == "pe"):
                self._wait(e, b.w)
        for b in writes:
            if b.w is not None and not (e == "pe" and b.w[0] == "pe"):
                self._wait(e, b.w)
            for ev in b.r:
                if ev[0] != e:
                    self._wait(e, ev)

    def op(self, e, fn, reads=(), writes=(), inc=True):
        self._deps(e, reads, writes)
        ins = fn(self.eng[e])
        self.ninst += 1
        if inc:
            self.cnt[e] += 1
            ins.then_inc(self.sems[e], 1)
            ev = (e, self.cnt[e])
        else:
            ev = (e, self.cnt[e] + 1)
        for b in reads:
            b.r.append(ev)
            if len(b.r) > 64:
                b.r = b.r[-48:] if False else b.r
        for b in writes:
            b.w = ev
            b.r = []
        return ins

    def dma(self, q, out, in_, reads=(), writes=(), **kw):
        self._deps(q, reads, writes)
        tgt = writes[0] if writes else reads[0]
        key = self._dsem(tgt)
        ins = self.eng[q].dma_start(out=out, in_=in_, **kw)
        ins.then_inc(self.sems[key], 16)
        self.cnt[key] += 16
        ev = (key, self.cnt[key])
        self.ninst += 1
        for b in reads:
            b.r.append(ev)
        for b in writes:
            b.w = ev
            b.r = []
        return ev

    def barrier(self, engines=("pe", "act", "dve", "sp")):
        for e in engines:
            for o in ("pe", "act", "dve", "pool"):
                if o != e and self.cnt[o] > 0:
                    self._wait(e, (o, self.cnt[o]))

    def act(self, out, in_, func, reads, writes, **kw):
        return self.op("act", lambda e: e.activation(out=out, in_=in_, func=func, **kw), reads, writes)

    def tt(self, eng, out, in0, in1, op, reads, writes):
        return self.op(eng, lambda e: e.tensor_tensor(out=out, in0=in0, in1=in1, op=op), reads, writes)

    def ts(self, eng, out, in0, s1, s2, op0, op1, reads, writes):
        return self.op(eng, lambda e: e.tensor_scalar(out=out, in0=in0, scalar1=s1, scalar2=s2, op0=op0, op1=op1),
                       reads, writes)

    def stt(self, eng, out, in0, scalar, in1, op0, op1, reads, writes):
        return self.op(eng, lambda e: e.scalar_tensor_tensor(out=out, in0=in0, scalar=scalar, in1=in1, op0=op0, op1=op1),
                       reads, writes)

    def mm(self, out, lhsT, rhs, start, stop, reads, writes):
        return self.op("pe", lambda e: e.matmul(out, lhsT=lhsT, rhs=rhs, start=start, stop=stop),
                       reads, writes, inc=stop)

    def copy(self, eng, out, in_, reads, writes):
        return self.op(eng, lambda e: e.tensor_copy(out=out, in_=in_), reads, writes)


class WStream:
    def __init__(self, k, plan):
        self.k = k
        self.plan = plan
        self.ws = k.sbuf("w_stage", [128, NS_W * 2048], F32)
        self.wb = k.sbuf("w_ring", [128, NB_W * 2048], BF16)
        self.sbufs = [Buf("ws%d" % i) for i in range(NS_W)]
        self.bbufs = [Buf("wb%d" % i) for i in range(NB_W)]
        self.loaded = 0
        self.released = 0
        self.paused = False

    def _load(self, i):
        k = self.k
        tag, src, shape3 = self.plan[i]
        s = i % NS_W
        b = i % NB_W
        st = self.ws[:, s * 2048:(s + 1) * 2048]
        if shape3 is not None:
            st = st.rearrange("p (a b) -> p a b", a=shape3[0])
        k.dma("sp", st, src, writes=[self.sbufs[s]])
        k.copy("pool", self.wb[:, b * 2048:(b + 1) * 2048], self.ws[:, s * 2048:(s + 1) * 2048],
               reads=[self.sbufs[s]], writes=[self.bbufs[b]])

    def pump(self):
        if self.paused:
            return
        while self.loaded < len(self.plan) and self.loaded < self.released + NB_W:
            self._load(self.loaded)
            self.loaded += 1

    def get(self, i, tag):
        assert self.plan[i][0] == tag, (i, tag, self.plan[i][0])
        self.pump()
        assert i < self.loaded, (i, self.loaded, self.released)
        b = i % NB_W
        return self.wb[:, b * 2048:(b + 1) * 2048], self.bbufs[b]

    def release(self, i):
        assert i == self.released, (i, self.released)
        self.released += 1
        self.pump()


def build_program(mode):
    nc = bass.Bass("TRN2", target_bir_lowering=False)
    dt = nc.dram_tensor
    x_d = dt("x", [T, D], F32, kind="ExternalInput").ap()
    gT_d = dt("gT", [128, 4 * KC], F32, kind="ExternalInput").ap()
    lbz_d = dt("lbz", [128, 3 * 16], F32, kind="ExternalInput").ap()
    hw_in_d = dt("hw_in", [D, 4 * D], F32, kind="ExternalInput").ap()
    if mode == "pre":
        aout_d = dt("aout", [128, 16], F32, kind="ExternalOutput").ap()
        bout_d = dt("bout", [128, 16 * 128], F32, kind="ExternalOutput").ap()
    else:
        aall_d = dt("aall", [128, NCORES * 16], F32, kind="ExternalInput").ap()
        ball_d = dt("ball", [128, 16 * NCORES * 128], F32, kind="ExternalInput").ap()
        cmask_d = dt("cmask", [128, NCORES], F32, kind="ExternalInput").ap()
        gn_d = dt("gn", [128, 1], F32, kind="ExternalInput").ap()
        hw_out_d = dt("hw_out", [D, D], F32, kind="ExternalInput").ap()
        w1_d = [dt("w1_%d" % i, [D, DFF], F32, kind="ExternalInput").ap() for i in range(2)]
        w2_d = [dt("w2_%d" % i, [DFF, D], F32, kind="ExternalInput").ap() for i in range(2)]
        gw_in_d = dt("gw_in", [D, 2 * D], F32, kind="ExternalInput").ap()
        gw_out_d = dt("gw_out", [D, D], F32, kind="ExternalInput").ap()
        lnT_d = dt("lnT", [128, 32], F32, kind="ExternalInput").ap()
        wsp_d = dt("wsp", [128, 16 * 128], F32, kind="ExternalInput").ap()
        bsp_d = dt("bsp", [1, 16 * 128], F32, kind="ExternalInput").ap()
        fin_d = dt("fin", [1, D], F32, kind="ExternalInput").ap()
        y_d = dt("y", [T, D], F32, kind="ExternalOutput").ap()
    full = mode == "main"

    plan = []

    def kmajor(w_ap, c0):
        return w_ap[:, c0:c0 + 128].rearrange("(kc k) c -> k kc c", k=128), (16, 128)

    def kmajor4(w_ap, kcg, c0):
        return w_ap[kcg * 512:(kcg + 1) * 512, c0:c0 + 512].rearrange("(kc k) c -> k kc c", k=128), (4, 512)

    def rows(w_ap, r0):
        return w_ap[r0:r0 + 128, :], None

    for h in range(16):
        for nm, blk in (("f", 1), ("q", 0), ("i", 2), ("g", 3)):
            if not full and nm in ("q", "g"):
                continue
            src, sh = kmajor(hw_in_d, blk * D + h * 128)
            plan.append(("hin_%s%d" % (nm, h), src, sh))
    if full:
        for h in range(16):
            src, sh = rows(hw_out_d, h * 128)
            plan.append(("hout%d" % h, src, sh))

        def plan_mlp(l):
            def p1(g):
                for j in range(4):
                    src, sh = kmajor(w1_d[l], (g * 4 + j) * 128)
                    plan.append(("w1_%d_%d" % (l, g * 4 + j), src, sh))

            def p2(g):
                for j in range(4):
                    src, sh = rows(w2_d[l], (g * 4 + j) * 128)
                    plan.append(("w2_%d_%d" % (l, g * 4 + j), src, sh))
            p1(0)
            for g in range(16):
                if g + 1 < 16:
                    p1(g + 1)
                p2(g)
        plan_mlp(0)
        for jb in range(4):
            for kcg in range(4):
                src, sh = kmajor4(gw_in_d, kcg, D + jb * 512)
                plan.append(("gv%d_%d" % (jb, kcg), src, sh))
        for g in range(16):
            src, sh = kmajor(gw_in_d, g * 128)
            plan.append(("gu%d" % g, src, sh))
        for g in range(16):
            src, sh = rows(gw_out_d, g * 128)
            plan.append(("gout%d" % g, src, sh))
        plan_mlp(1)

    with ExitStack() as es:
        k = K(nc, es)
        Hs = k.sbuf("H", [128, NT * D], F32)
        HNs = k.sbuf("HN", [128, KC * T], BF16)
        Cs = k.sbuf("C", [128, 16 * T], BF16)
        ws = WStream(k, plan)
        ident = k.sbuf("ident", [128, 128], BF16)
        bdm = k.sbuf("bdm", [128, 128], F32)
        rmask = k.sbuf("rmask", [128, T], F32)
        cm4 = k.sbuf("cm4", [128, 4], F32)
        gT = k.sbuf("gTs", [128, 4 * KC], F32)
        lbt = k.sbuf("lbt", [128, 8 * 16], F32)
        epsA = k.sbuf("epsA", [128, 1], F32)
        st4 = k.sbuf("st4", [128, 16], F32)
        ones32 = k.sbuf("ones32", [128, 128], F32)
        onesb = k.sbuf("onesb", [128, 128], BF16)
        Sf = k.sbuf("Sf", [128, 128], F32)
        Sb = k.sbuf("Sb", [128, 8 * 128], BF16)
        ach = k.sbuf("ach", [128, 32], F32)
        B_H = [Buf("H%d" % i) for i in range(NT)]
        B_HN = Buf("HN")
        B_C = [Buf("C%d" % i) for i in range(16)]
        B_const = Buf("const")
        B_st = Buf("st4")
        B_Sf = Buf("Sf")
        B_Sb = [Buf("Sb%d" % i) for i in range(8)]
        B_ach = Buf("ach")
        aseg = k.sbuf("aseg", [128, 16], F32)
        ach2 = k.sbuf("ach2", [128, 32], F32)
        B_aseg = Buf("aseg")
        PS = [k.psum("ps%d" % i, [128, 512], F32) for i in range(8)]
        B_PS = [Buf("ps%d" % i) for i in range(8)]
        rot = {"a": [0, 1, 2, 3], "b": [4, 5], "c": [6, 7]}
        import os as _os
        if _os.environ.get("KBANKS"):
            rot = {kv.split(":")[0]: [int(v) for v in kv.split(":")[1].split(",")] for kv in _os.environ["KBANKS"].split(";")}
        rotc = {"a": 0, "b": 0, "c": 0}

        def bank(cls):
            lst = rot[cls]
            i = lst[rotc[cls] % len(lst)]
            rotc[cls] += 1
            return PS[i], B_PS[i]

        Hv = Hs[:].rearrange("p (t d) -> p t d", t=NT)
        HNv = HNs[:].rearrange("p (c t) -> p c t", c=KC)
        Cv = Cs[:].rearrange("p (c t) -> p c t", c=16)

        k.op("pool", lambda e: e.memset(ident[:], 1.0), writes=[B_const])
        k.op("pool", lambda e: e.affine_select(out=ident[:], in_=ident[:], pattern=[[-1, 128]],
                                               compare_op=ALU.is_equal, fill=0.0, base=0, channel_multiplier=1),
             reads=[B_const], writes=[B_const])
        k.op("pool", lambda e: e.memset(bdm[:], 1.0), writes=[B_const])
        k.op("pool", lambda e: e.affine_select(out=bdm[:], in_=bdm[:], pattern=[[1, 128]],
                                               compare_op=ALU.is_ge, fill=0.0, base=0, channel_multiplier=-1),
             reads=[B_const], writes=[B_const])
        for c in range(1, 4):
            k.op("pool", lambda e, c=c: e.memset(bdm[0:32 * c, 32 * c:32 * (c + 1)], 0.0), reads=[B_const], writes=[B_const])
        k.op("pool", lambda e: e.memset(rmask[:], 1.0), writes=[B_const])
        k.op("pool", lambda e: e.memset(rmask[:].rearrange("p (c t) -> p c t", t=32)[:, :, 0:1], 0.0),
             reads=[B_const], writes=[B_const])
        k.op("pool", lambda e: e.memset(cm4[:], 0.0), writes=[B_const])
        for c in range(3):
            pass
        k.op("pool", lambda e: e.memset(cm4[0:32, 0:1], 1.0), reads=[B_const], writes=[B_const])
        k.op("pool", lambda e: e.memset(cm4[32:64, 1:2], 1.0), reads=[B_const], writes=[B_const])
        k.op("pool", lambda e: e.memset(cm4[64:96, 2:3], 1.0), reads=[B_const], writes=[B_const])
        k.op("pool", lambda e: e.memset(cm4[:, 3:4], 1.0), reads=[B_const], writes=[B_const])
        k.op("pool", lambda e: e.memset(cm4[0:96, 3:4], 0.0), reads=[B_const], writes=[B_const])
        k.op("pool", lambda e: e.memset(epsA[:], EPS), writes=[B_const])
        k.op("pool", lambda e: e.memset(ones32[:], 1.0), writes=[B_const])
        k.op("pool", lambda e: e.memset(onesb[:], 1.0), writes=[B_const])
        k.dma("sp", gT[:], gT_d, writes=[B_const])
        k.dma("sp", lbt[:, 0:48], lbz_d, writes=[B_const])
        z0, z1, z2 = lbt[:, 0:16], lbt[:, 16:32], lbt[:, 32:48]
        mx = lbt[:, 48:64]
        k.tt("dve", mx, z0, z1, ALU.max, [B_const], [B_const])
        k.tt("dve", mx, mx, z2, ALU.max, [B_const], [B_const])
        for zz in (z0, z1, z2):
            k.tt("dve", zz, zz, mx, ALU.subtract, [B_const], [B_const])
        k.act(lbt[:, 0:48], lbt[:, 0:48], AF.Exp, [B_const], [B_const])
        sm = lbt[:, 64:80]
        k.tt("dve", sm, z0, z1, ALU.add, [B_const], [B_const])
        k.tt("dve", sm, sm, z2, ALU.add, [B_const], [B_const])
        k.op("dve", lambda e: e.reciprocal(out=sm, in_=sm), [B_const], [B_const])
        LB = lbt[:, 96:112]
        OML = lbt[:, 112:128]
        k.tt("dve", LB, z0, sm, ALU.mult, [B_const], [B_const])
        k.ts("dve", OML, LB, -1.0, 1.0, ALU.mult, ALU.add, [B_const], [B_const])

        import os
        STOP = int(os.environ.get("KSTOP", "0"))
        if STOP == 1:
            k.barrier(("sp",))
            return nc
        def load_x():
            for t in range(NT):
                k.dma("sp", Hv[:, t, :], x_d[t * 128:(t + 1) * 128, :], writes=[B_H[t]])

        def rms_to_HN(gi, junk_ap, junk_buf, xs_ap, xs_buf):
            for t in range(NT):
                src = Hv[:, t, :]
                ss = st4[:, 0:1]
                k.act(junk_ap, src, AF.Square, [B_H[t]], [junk_buf, B_st], accum_out=ss)
                k.act(st4[:, 1:2], ss, AF.Sqrt, [B_st, B_const], [B_st], bias=epsA[:, 0:1], scale=1.0 / D)
                k.op("dve", lambda e: e.reciprocal(out=st4[:, 2:3], in_=st4[:, 1:2]), [B_st], [B_st])
                k.act(xs_ap, src, AF.Copy, [B_H[t], B_st], [xs_buf], scale=st4[:, 2:3])
                for hf in range(2):
                    pb, Bp = bank("a")
                    pbb = pb[:].bitcast(BF16)
                    for j in range(8):
                        kc = hf * 8 + j
                        k.op("pe", lambda e, kc=kc, j=j: e.transpose(out=pbb[:, j * 128:(j + 1) * 128],
                                                                      in_=xs_ap[:, kc * 128:(kc + 1) * 128], identity=ident[:]),
                             reads=[xs_buf, B_const], writes=[Bp], inc=(j == 7))
                    g_b = gT[:, gi * KC + hf * 8: gi * KC + hf * 8 + 8].unsqueeze(2).to_broadcast([128, 8, 128])
                    k.tt("dve", HNv[:, hf * 8:(hf + 1) * 8, t * 128:(t + 1) * 128],
                         pbb[:, 0:1024].rearrange("p (c t) -> p c t", c=8), g_b, ALU.mult,
                         [Bp, B_const], [B_HN])

        def acc_into_H(base, tags, lhsT_of, lhs_bufs_of):
            chunks = [ws.get(base + j, tags[j]) for j in range(4)]
            for t in range(NT):
                for nb in range(4):
                    pb, Bp = bank("b")
                    for j in range(4):
                        wap, wbuf = chunks[j]
                        k.mm(pb[:], lhsT_of(j, t), wap[:, nb * 512:(nb + 1) * 512], j == 0, j == 3,
                             [wbuf] + lhs_bufs_of(j), [Bp])
                    hs = Hv[:, t, nb * 512:(nb + 1) * 512]
                    k.tt("dve", hs, pb[:], hs, ALU.add, [Bp, B_H[t]], [B_H[t]])
            for j in range(4):
                ws.release(base + j)

        junk_ap = ws.ws[:, 0:1024].bitcast(BF16)
        xs_ap = ws.ws[:, 2048:3072].bitcast(BF16)

        load_x()
        if STOP == 2:
            k.barrier(("sp",))
            return nc
        rms_to_HN(0, junk_ap, ws.sbufs[0], xs_ap, ws.sbufs[1])
        k.barrier()
        if STOP == 3:
            k.barrier(("sp",))
            return nc

        HW = Hs
        def hslot(i, n=1):
            return HW[:, i * 1024:(i + n) * 1024]
        sig = hslot(0); lf = hslot(1); b32 = hslot(2); Ee = hslot(3); En = hslot(4); kk = hslot(5)
        qs = hslot(6); gate = hslot(7); o_sb = hslot(8); osq = hslot(9)
        bfr = HW[:, 10 * 1024:16 * 1024].bitcast(BF16)
        Qt = bfr[:, 0:1024]; Kt = bfr[:, 1024:2048]; Kh = bfr[:, 2048:3072]; vT = bfr[:, 3072:4096]
        vtok = bfr[:, 4096:5120]; khtok = bfr[:, 5120:6144]
        vblk = bfr[:, 6144:7168]
        sctm = bfr[:, 7168:7424]
        bst = HW[:, 14 * 1024:15 * 1024]
        names = ["sig", "lf", "b32", "Ee", "En", "kk", "qs", "gate", "o_sb", "osq", "Qt", "Kt", "Kh", "vT",
                 "vtok", "khtok", "vblk0", "vblk1", "sctm0", "sctm1", "bst"]
        BH = {n: Buf(n) for n in names}
        B_vtok = [Buf("vtok%d" % i) for i in range(NT)]
        B_khtok = [Buf("khtok%d" % i) for i in range(NT)]

        if full:
            aall = k.sbuf("aall", [128, NCORES * 16], F32)
            cmask = k.sbuf("cmask", [128, NCORES], F32)
            gn = k.sbuf("gn", [128, 1], F32)
            k.dma("sp", aall[:], aall_d, writes=[B_const])
            k.dma("sp", cmask[:], cmask_d, writes=[B_const])
            k.dma("sp", gn[:], gn_d, writes=[B_const])
            omm = st4[:, 8:16]
            k.ts("dve", omm, cmask[:], -1.0, 1.0, ALU.mult, ALU.add, [B_const], [B_st])
            for r in range(NCORES):
                k.ts("dve", aall[:, r * 16:(r + 1) * 16], aall[:, r * 16:(r + 1) * 16], cmask[:, r:r + 1], omm[:, r:r + 1],
                     ALU.mult, ALU.add, [B_const, B_st], [B_const])

        def proj_fm(widx, tag, evac):
            wap, wbuf = ws.get(widx, tag)
            w3 = wap.rearrange("p (a b) -> p a b", a=16)
            for hf in range(2):
                pb, Bp = bank("a")
                for kc in range(KC):
                    k.mm(pb[:], w3[:, kc, :], HNv[:, kc, hf * 512:(hf + 1) * 512], kc == 0, kc == KC - 1,
                         [wbuf, B_HN], [Bp])
                evac(hf, pb, Bp)
            ws.release(widx)

        widx = 0
        for h in range(16):
            lbh = LB[:, h:h + 1]
            omlh = OML[:, h:h + 1]
            proj_fm(widx, "hin_f%d" % h,
                    lambda hf, pb, Bp: k.act(sig[:, hf * 512:(hf + 1) * 512], pb[:], AF.Sigmoid, [Bp], [BH["sig"]]))
            widx += 1
            k.ts("dve", sig, sig, omlh, lbh, ALU.mult, ALU.add, [BH["sig"], B_const], [BH["sig"]])
            k.act(lf, sig, AF.Ln, [BH["sig"]], [BH["lf"]])
            k.ts("dve", kk, sig, -1.0, 1.0, ALU.mult, ALU.add, [BH["sig"]], [BH["kk"]])
            k.op("dve", lambda e: e.tensor_tensor_scan(out=b32, data0=rmask[:], data1=lf, initial=0.0,
                                                       op0=ALU.mult, op1=ALU.add),
                 [B_const, BH["lf"]], [BH["b32"]])
            b3 = b32.rearrange("p (c t) -> p c t", t=32)
            blast = b3[:, :, 31:32]
            k.act(ach[:].unsqueeze(2), blast, AF.Exp, [BH["b32"]], [B_ach])
            k.act(En, b32, AF.Exp, [BH["b32"]], [BH["En"]], scale=-1.0)
            k.tt("dve", lf.rearrange("p (c t) -> p c t", t=32), b3, blast.to_broadcast([128, 32, 32]), ALU.subtract,
                 [BH["b32"], BH["lf"]], [BH["lf"]])
            k.act(lf, lf, AF.Exp, [BH["lf"]], [BH["lf"]], scale=-1.0)
            k.tt("dve", Kh, kk, lf, ALU.mult, [BH["kk"], BH["lf"]], [BH["Kh"]])
            if STOP == 4:
                k.barrier(("sp",))
                return nc
            if full:
                k.act(Ee, b32, AF.Exp, [BH["b32"]], [BH["Ee"]])
                k.tt("dve", Kt, kk, En, ALU.mult, [BH["kk"], BH["En"]], [BH["Kt"]])
                proj_fm(widx, "hin_q%d" % h,
                        lambda hf, pb, Bp: k.act(qs[:, hf * 512:(hf + 1) * 512], pb[:], AF.Silu, [Bp], [BH["qs"]]))
                widx += 1
                k.tt("dve", Qt, qs, Ee, ALU.mult, [BH["qs"], BH["Ee"]], [BH["Qt"]])
            proj_fm(widx, "hin_i%d" % h,
                    lambda hf, pb, Bp: k.act(vT[:, hf * 512:(hf + 1) * 512], pb[:], AF.Copy, [Bp], [BH["vT"]]))
            widx += 1
            if full:
                proj_fm(widx, "hin_g%d" % h,
                        lambda hf, pb, Bp: k.act(gate[:, hf * 512:(hf + 1) * 512], pb[:], AF.Silu, [Bp], [BH["gate"]]))
                widx += 1
            if full:
                k.dma("sp", bst.rearrange("p (r e) -> p r e", r=NCORES),
                      ball_d[:, h * NCORES * 128:(h + 1) * NCORES * 128].rearrange("p (r e) -> p r e", r=NCORES),
                      writes=[BH["bst"]])
                k.op("dve", lambda e: e.memset(Sf[:], 0.0), writes=[B_Sf])
                for r in range(NCORES):
                    br = bst[:, r * 128:(r + 1) * 128]
                    k.ts("dve", br, br, cmask[:, r:r + 1], 1.0, ALU.mult, ALU.mult, [BH["bst"], B_const], [BH["bst"]])
                    k.stt("dve", Sf[:], Sf[:], aall[:, r * 16 + h:r * 16 + h + 1], br, ALU.mult, ALU.add,
                          [B_Sf, BH["bst"], B_const], [B_Sf])
                k.act(Sb[:, 0:128], Sf[:], AF.Copy, [B_Sf], [B_Sb[0]])
            else:
                k.op("dve", lambda e: e.memset(Sf[:], 0.0), writes=[B_Sf])
            if full:
                obanks = [bank("b"), bank("b")]
            for t in range(NT):
                tsl = slice(t * 128, (t + 1) * 128)
                pb, Bp = bank("c")
                pbb = pb[:].bitcast(BF16)
                k.op("pe", lambda e: e.transpose(out=pbb[:, 0:128], in_=vT[:, tsl], identity=ident[:]),
                     [BH["vT"], B_const], [Bp], inc=False)
                k.op("pe", lambda e: e.transpose(out=pbb[:, 128:256], in_=Kh[:, tsl], identity=ident[:]),
                     [BH["Kh"], B_const], [Bp], inc=True)
                vb = vblk[:, (t % 2) * 512:(t % 2 + 1) * 512]
                Bvb = BH["vblk%d" % (t % 2)]
                for c4 in range(4):
                    k.tt("dve", vb[:, c4 * 128:(c4 + 1) * 128], pbb[:, 0:128], cm4[:, c4:c4 + 1].to_broadcast([128, 128]),
                         ALU.mult, [Bp, B_const], [Bvb])
                k.copy("dve", khtok[:, tsl], pbb[:, 128:256], [Bp], [B_khtok[t]])
                if full:
                    k.copy("dve", vtok[:, tsl], pbb[:, 0:128], [Bp], [B_vtok[t]])
                dsb, Bds = bank("c")
                k.mm(dsb[:], khtok[:, tsl], vb, True, True, [B_khtok[t], Bvb], [Bds])
                if STOP == 6:
                    k.barrier(("sp",))
                    return nc
                if full:
                    sb_, Bsc = bank("a")
                    k.mm(sb_[:, 0:128], Kt[:, tsl], Qt[:, tsl], True, True, [BH["Kt"], BH["Qt"]], [Bsc])
                    sm_ = sctm[:, (t % 2) * 128:(t % 2 + 1) * 128]
                    Bsm = BH["sctm%d" % (t % 2)]
                    k.tt("dve", sm_, sb_[:, 0:128], bdm[:], ALU.mult, [Bsc, B_const], [Bsm])
                    ob, Bob = obanks[t // 4]
                    oc = (t % 4) * 128
                    k.mm(ob[:, oc:oc + 128], vtok[:, tsl], sm_, True, False, [B_vtok[t], Bsm], [Bob])
                for c in range(4):
                    gc = t * 4 + c
                    if full:
                        k.mm(ob[:, oc + c * 32:oc + (c + 1) * 32], Sb[:, (gc % 8) * 128:(gc % 8 + 1) * 128],
                             Qt[:, gc * 32:(gc + 1) * 32], False, c == 3, [B_Sb[gc % 8], BH["Qt"]], [Bob])
                    k.ts("dve", Sf[:], Sf[:], ach[:, gc:gc + 1], 1.0, ALU.mult, ALU.mult, [B_Sf, B_ach], [B_Sf])
                    k.tt("dve", Sf[:], dsb[:, c * 128:(c + 1) * 128], Sf[:], ALU.add, [B_Sf, Bds], [B_Sf])
                    if full and gc + 1 < 32:
                        n = (gc + 1) % 8
                        k.act(Sb[:, n * 128:(n + 1) * 128], Sf[:], AF.Copy, [B_Sf], [B_Sb[n]])
            if STOP == 7:
                k.barrier(("sp",))
                return nc
            if not full:
                k.act(ach2[:], b3[:, :, 31], AF.Copy, [BH["b32"]], [B_st], accum_out=st4[:, 4:5])
                k.act(aseg[:, h:h + 1], st4[:, 4:5], AF.Exp, [B_st], [B_aseg])
                k.dma("sp", bout_d[:, h * 128:(h + 1) * 128], Sf[:], reads=[B_Sf])
            else:
                for hf in range(2):
                    ob, Bob = obanks[hf]
                    hsl = slice(hf * 512, (hf + 1) * 512)
                    k.act(o_sb[:, hsl], ob[:], AF.Copy, [Bob], [BH["o_sb"]])
                    osqb = osq.bitcast(BF16)[:, hf * 512:(hf + 1) * 512]
                    k.act(osqb, o_sb[:, hsl], AF.Square, [BH["o_sb"]], [BH["osq"]])
                    pb, Bp = bank("a")
                    k.mm(pb[:], onesb[:], osqb, True, True, [B_const, BH["osq"]], [Bp])
                    rsq = sig[:, hsl]
                    k.act(rsq, pb[:], AF.Sqrt, [Bp, B_const], [BH["sig"]], bias=epsA[:, 0:1], scale=1.0 / 128)
                    k.op("dve", lambda e: e.reciprocal(out=rsq, in_=rsq), [BH["sig"]], [BH["sig"]])
                    k.tt("dve", o_sb[:, hsl], o_sb[:, hsl], rsq, ALU.mult, [BH["o_sb"], BH["sig"]], [BH["o_sb"]])
                    k.stt("dve", Cv[:, h, hsl], o_sb[:, hsl], gn[:, 0:1], gate[:, hsl], ALU.mult, ALU.mult,
                          [BH["o_sb"], BH["gate"], B_const], [B_C[h]])

        if not full:
            k.dma("sp", aout_d, aseg[:], reads=[B_aseg])
            k.barrier(("sp",))
            for b in [B_aseg, B_Sf]:
                if b.dsem is not None:
                    k._wait("sp", (b.dsem, k.cnt[b.dsem]))
            return nc

        k.barrier()
        load_x()
        for g in range(4):
            acc_into_H(widx, ["hout%d" % (g * 4 + j) for j in range(4)],
                       lambda j, t, g=g: Cv[:, g * 4 + j, t * 128:(t + 1) * 128],
                       lambda j, g=g: [B_C[g * 4 + j]])
            widx += 4

        def mlp(l, gi):
            nonlocal widx
            k.barrier()
            rms_to_HN(gi, junk_ap, ws.sbufs[0], xs_ap, ws.sbufs[1])

            def p1(g):
                nonlocal widx
                for j in range(4):
                    fc = g * 4 + j
                    slot = fc % 16

                    def ev(hf, pb, Bp, slot=slot):
                        dst = Cv[:, slot, hf * 512:(hf + 1) * 512]
                        k.act(dst, pb[:], AF.Relu, [Bp], [B_C[slot]])
                        k.tt("dve", dst, pb[:], dst, ALU.mult, [Bp, B_C[slot]], [B_C[slot]])
                    proj_fm(widx, "w1_%d_%d" % (l, fc), ev)
                    widx += 1

            def p2(g):
                nonlocal widx
                acc_into_H(widx, ["w2_%d_%d" % (l, g * 4 + j) for j in range(4)],
                           lambda j, t, g=g: Cv[:, (g * 4 + j) % 16, t * 128:(t + 1) * 128],
                           lambda j, g=g: [B_C[(g * 4 + j) % 16]])
                widx += 4
            p1(0)
            for g in range(16):
                if g + 1 < 16:
                    p1(g + 1)
                p2(g)

        mlp(0, 1)

        k.barrier()
        rms_to_HN(2, junk_ap, ws.sbufs[0], xs_ap, ws.sbufs[1])
        lnT = k.sbuf("lnT", [128, 32], F32)
        WTs = k.sbuf("WTs", [128, 16 * 128], BF16)
        B2s = k.sbuf("B2s", [128, 16 * 128], F32)
        ug = k.sbuf("ug", [128, 2 * 512], BF16)
        tmpx = k.sbuf("tmpx", [128, 256], F32)
        B_tmpx = Buf("tmpx")
        B_ug = [Buf("ug0"), Buf("ug1")]
        B_g = Buf("gconst")
        k.dma("sp", lnT[:], lnT_d, writes=[B_g])
        k.dma("sp", B2s[:], bsp_d.partition_broadcast(128), writes=[B_g])
        wsp32 = Cs[:, 0:4096].bitcast(F32)
        k.dma("sp", wsp32, wsp_d, writes=[B_C[0], B_C[1], B_C[2], B_C[3]])
        wspb = Cs[:, 4096:6144]
        k.copy("dve", wspb, wsp32, [B_C[0], B_C[1], B_C[2], B_C[3]], [B_C[4], B_C[5]])
        for hf in range(2):
            pb, Bp = bank("a")
            pbb = pb[:].bitcast(BF16)
            for j in range(8):
                g = hf * 8 + j
                k.op("pe", lambda e, g=g, j=j: e.transpose(out=pbb[:, j * 128:(j + 1) * 128],
                                                            in_=wspb[:, g * 128:(g + 1) * 128], identity=ident[:]),
                     [B_C[4], B_C[5], B_const], [Bp], inc=(j == 7))
            k.copy("dve", WTs[:, hf * 1024:(hf + 1) * 1024], pbb[:, 0:1024], [Bp], [B_g])
        WT3 = WTs[:].rearrange("p (g t) -> p g t", g=16)
        k.op("dve", lambda e: e.memset(WT3[64:128, :, 0:64], 0.0), [B_g], [B_g])
        for q4 in range(4):
            pb, Bp = bank("a")
            k.mm(pb[:], onesb[:], WTs[:, q4 * 512:(q4 + 1) * 512], True, True, [B_const, B_g], [Bp])
            for j in range(4):
                g = q4 * 4 + j
                k.tt("dve", tmpx[:, 0:128], pb[:, j * 128:(j + 1) * 128], lnT[:, 16 + g:17 + g].to_broadcast([128, 128]),
                     ALU.mult, [Bp, B_g], [B_tmpx])
                k.tt("dve", B2s[:, g * 128:(g + 1) * 128], tmpx[:, 0:128], B2s[:, g * 128:(g + 1) * 128], ALU.add,
                     [B_tmpx, B_g], [B_g])
        k.barrier()
        Vv = Cs[:].rearrange("p (t f) -> p t f", t=NT)
        B_V = [Buf("V%d" % i) for i in range(NT)]
        for jb in range(4):
            chunks = [ws.get(widx + q, "gv%d_%d" % (jb, q)) for q in range(4)]
            for t in range(NT):
                pb, Bp = bank("a")
                for kc in range(KC):
                    wap, wbuf = chunks[kc // 4]
                    w3 = wap.rearrange("p (a b) -> p a b", a=4)
                    k.mm(pb[:], HNv[:, kc, t * 128:(t + 1) * 128], w3[:, kc % 4, :], kc == 0, kc == KC - 1,
                         [wbuf, B_HN], [Bp])
                k.act(Vv[:, t, jb * 512:(jb + 1) * 512], pb[:], AF.Gelu, [Bp], [B_V[t]])
            for q in range(4):
                ws.release(widx + q)
            widx += 4
        ws.paused = True
        for t in range(NT):
            vt = Vv[:, t, :]
            k.act(junk_ap, vt, AF.Copy, [B_V[t]], [ws.sbufs[0], B_st], accum_out=st4[:, 0:1])
            k.act(junk_ap, vt, AF.Square, [B_V[t]], [ws.sbufs[0], B_st], accum_out=st4[:, 1:2])
            k.ts("dve", st4[:, 2:3], st4[:, 0:1], 1.0 / D, 1.0, ALU.mult, ALU.mult, [B_st], [B_st])
            k.tt("dve", st4[:, 3:4], st4[:, 2:3], st4[:, 2:3], ALU.mult, [B_st], [B_st])
            k.stt("dve", st4[:, 4:5], st4[:, 1:2], 1.0 / D, st4[:, 3:4], ALU.mult, ALU.subtract, [B_st], [B_st])
            k.act(st4[:, 5:6], st4[:, 4:5], AF.Sqrt, [B_st, B_const], [B_st], bias=epsA[:, 0:1], scale=1.0)
            k.op("dve", lambda e: e.reciprocal(out=st4[:, 6:7], in_=st4[:, 5:6]), [B_st], [B_st])
            k.stt("dve", st4[:, 7:8], st4[:, 2:3], -1.0, st4[:, 6:7], ALU.mult, ALU.mult, [B_st], [B_st])
            k.ts("dve", vt, vt, st4[:, 6:7], st4[:, 7:8], ALU.mult, ALU.add, [B_V[t], B_st], [B_V[t]])
            banks4 = [bank("a") for _ in range(4)]
            for g in range(16):
                pb, Bp = banks4[g // 4]
                k.mm(pb[:, (g % 4) * 128:(g % 4 + 1) * 128], Vv[:, t, g * 128:(g + 1) * 128], WT3[:, g, :], True, True,
                     [B_V[t], B_g], [Bp])
            for g in range(16):
                pb, Bp = banks4[g // 4]
                tx = tmpx[:, (g % 2) * 128:(g % 2 + 1) * 128]
                k.tt("dve", tx, pb[:, (g % 4) * 128:(g % 4 + 1) * 128], lnT[:, g:g + 1].to_broadcast([128, 128]),
                     ALU.mult, [Bp, B_g], [B_tmpx])
                k.tt("dve", Vv[:, t, g * 128:(g + 1) * 128], tx, B2s[:, g * 128:(g + 1) * 128], ALU.add,
                     [B_tmpx, B_g, B_V[t]], [B_V[t]])
        ws.paused = False
        MX = Cs[:].rearrange("p (t g k) -> p t g k", t=NT, g=16)
        for g in range(16):
            def ev(hf, pb, Bp, g=g):
                u_ = ug[:, hf * 512:(hf + 1) * 512]
                k.act(u_, pb[:], AF.Gelu, [Bp], [B_ug[hf]])
                dst = MX[:, hf * 4:(hf + 1) * 4, g, :]
                k.tt("dve", dst, dst, u_.rearrange("p (t k) -> p t k", t=4), ALU.mult,
                     [B_ug[hf]] + B_V[hf * 4:(hf + 1) * 4], B_V[hf * 4:(hf + 1) * 4])
            proj_fm(widx, "gu%d" % g, ev)
            widx += 1
        for gg in range(4):
            acc_into_H(widx, ["gout%d" % (gg * 4 + j) for j in range(4)],
                       lambda j, t, gg=gg: MX[:, t, gg * 4 + j, :],
                       lambda j: list(B_V))
            widx += 4
        k.barrier()

        mlp(1, 3)

        k.barrier()
        Gf = Cs[:, 0:4096].bitcast(F32)
        k.dma("sp", Gf, fin_d.partition_broadcast(128), writes=[B_C[0], B_C[1], B_C[2], B_C[3]])
        outs = []
        for t in range(NT):
            src = Hv[:, t, :]
            k.act(junk_ap, src, AF.Square, [B_H[t]], [ws.sbufs[0], B_st], accum_out=st4[:, 0:1])
            k.act(st4[:, 1:2], st4[:, 0:1], AF.Sqrt, [B_st, B_const], [B_st], bias=epsA[:, 0:1], scale=1.0 / D)
            k.op("dve", lambda e: e.reciprocal(out=st4[:, 2:3], in_=st4[:, 1:2]), [B_st], [B_st])
            k.stt("dve", src, src, st4[:, 2:3], Gf, ALU.mult, ALU.mult, [B_H[t], B_st, B_C[0]], [B_H[t]])
            outs.append((k.dma("sp", y_d[t * 128:(t + 1) * 128, :], src, reads=[B_H[t]]), B_H[t]))
        for ev, b in outs:
            k._wait("sp", ev)
        assert widx == len(plan), (widx, len(plan))
        print("kernel build: ninst", k.ninst, "nwait", k.nwait, "dsems", k.ndsem)
    return nc


_PROG = {}


def _prog(mode):
    if mode not in _PROG:
        _PROG[mode] = build_program(mode)
    return _PROG[mode]


def kernel(x, norm_mix, norm_mlp, final_norm, hgrn_w_in, hgrn_w_out, hgrn_g_norm, hgrn_lb_logits,
           gmlp_w_in, gmlp_w_out, gmlp_ln_gain, gmlp_ln_bias, gmlp_w_spatial, gmlp_b_spatial, mlp_w1, mlp_w2):
    f = lambda a: np.ascontiguousarray(np.asarray(a, dtype=np.float32))
    x = f(x).reshape(NCORES, T, D)
    gains = [norm_mix[0], norm_mlp[0], norm_mix[1], norm_mlp[1]]
    gT = f(np.concatenate([np.asarray(g, np.float32).reshape(KC, 128).T for g in gains], axis=1))
    lbz = f(np.asarray(hgrn_lb_logits, np.float32).reshape(3, 16, 128).transpose(2, 0, 1).reshape(128, 48))
    hw_in = f(hgrn_w_in[0])
    cores = list(range(NCORES))
    pre = _prog("pre")
    in1 = [{"x": x[c], "gT": gT, "lbz": lbz, "hw_in": hw_in} for c in cores]
    r1 = run_bass_kernel_spmd(pre, in1, core_ids=cores).results
    aall = f(np.concatenate([r1[c]["aout"] for c in cores], axis=1))
    ball = f(np.stack([r1[c]["bout"].reshape(128, 16, 128) for c in cores], axis=2).reshape(128, 16 * NCORES * 128))
    main = _prog("main")
    lnT = f(np.concatenate([np.asarray(gmlp_ln_gain[0], np.float32).reshape(16, 128).T,
                            np.asarray(gmlp_ln_bias[0], np.float32).reshape(16, 128).T], axis=1))
    wsp = f(np.asarray(gmlp_w_spatial[0], np.float32).transpose(1, 0, 2).reshape(128, 16 * 128))
    bsp = f(np.asarray(gmlp_b_spatial[0], np.float32).reshape(1, 16 * 128))
    common = {
        "gT": gT, "lbz": lbz, "hw_in": hw_in, "aall": aall, "ball": ball,
        "gn": f(np.asarray(hgrn_g_norm[0], np.float32).reshape(128, 1)), "hw_out": f(hgrn_w_out[0]),
        "w1_0": f(mlp_w1[0]), "w1_1": f(mlp_w1[1]), "w2_0": f(mlp_w2[0]), "w2_1": f(mlp_w2[1]),
        "gw_in": f(gmlp_w_in[0]), "gw_out": f(gmlp_w_out[0]), "lnT": lnT, "wsp": wsp, "bsp": bsp,
        "fin": f(np.asarray(final_norm, np.float32).reshape(1, D)),
    }
    in2 = []
    for c in cores:
        m = dict(common)
        m["x"] = x[c]
        cm = np.zeros((128, NCORES), np.float32)
        cm[:, :c] = 1.0
        m["cmask"] = cm
        in2.append(m)
    r2 = run_bass_kernel_spmd(main, in2, core_ids=cores).results
    y = np.stack([r2[c]["y"] for c in cores], axis=0).reshape(1, NCORES * T, D)
    return y.astype(np.float32)
```
